# Optimizing a Trainium2 kernel written in Bass

```python
import jax, jax.numpy as jnp
from jax import lax
import numpy as np

D_MODEL = 1024
BATCH = 8
SEQ = 4096
DEPTH = 4

D_MIX = 2 * D_MODEL
SSD_WIDTH = D_MIX // 2
SSD_HEAD_DIM = 64
SSD_HEADS = SSD_WIDTH // SSD_HEAD_DIM
SSD_GROUPS = 2
SSD_HPG = SSD_HEADS // SSD_GROUPS
SSD_STATE = 128
SSD_CONV = 5
SSD_CHUNK = 64
XBC_WIDTH = SSD_WIDTH + 2 * SSD_GROUPS * SSD_STATE
HG_WIDTH = D_MIX - SSD_WIDTH
HG_EXPAND = 128
HG_HEADS = HG_WIDTH // HG_EXPAND
HG_DK = HG_EXPAND
HG_DV = HG_WIDTH // HG_HEADS
HG_CHUNK = 32
D_FF = 4 * D_MODEL
N_MOD = 6
EPS = 1e-6
LB_FLOOR = 1e-30

IN_SIZES = (SSD_WIDTH, XBC_WIDTH, SSD_HEADS, HG_WIDTH, HG_WIDTH, HG_WIDTH, HG_WIDTH, HG_WIDTH)
D_IN_PROJ = sum(IN_SIZES)
IN_SPLITS = tuple(np.cumsum(IN_SIZES)[:-1].tolist())

kernel_name = "hybrid_ssd_hgrn2_adaln_encoder"


def rmsnorm(x, w):
    xf = x.astype(jnp.float32)
    y = xf * lax.rsqrt(jnp.mean(xf * xf, axis=-1, keepdims=True) + EPS)
    return (y * w.astype(jnp.float32)).astype(x.dtype)


def to_chunks(a, chunk):
    b, l = a.shape[:2]
    return jnp.moveaxis(a.reshape(b, l // chunk, chunk, *a.shape[2:]), 1, 0)


def from_chunks(a):
    n, b, chunk = a.shape[:3]
    return jnp.moveaxis(a, 0, 1).reshape(b, n * chunk, *a.shape[3:])


def flip_seq(a):
    return jnp.flip(a, axis=1)


def masked_decay(seg, tri):
    return jnp.where(tri, jnp.exp(jnp.where(tri, seg, 0.0)), 0.0)


def depthwise_conv_centred(x, w):
    k = w.shape[0]
    return lax.conv_general_dilated(
        x, w[:, None, :].astype(x.dtype), window_strides=(1,),
        padding=[(k // 2, k // 2)], dimension_numbers=("NWC", "WIO", "NWC"),
        feature_group_count=x.shape[-1])


def ssd_direction(xh, bg, cg, dt, a_neg):
    bsz = xh.shape[0]
    log_a = dt * a_neg
    xs = tuple(to_chunks(t, SSD_CHUNK) for t in (xh, bg, cg, dt, log_a))
    tri = jnp.tril(jnp.ones((SSD_CHUNK, SSD_CHUNK), dtype=bool))[None, :, :, None, None]

    def step(state, inp):
        xc, bc, cc, dtc, ac = inp
        cum = jnp.cumsum(ac, axis=1)
        seg = cum[:, :, None] - cum[:, None, :]
        decay = masked_decay(seg, tri)
        scores = jnp.einsum("btgn,bsgn->btsg", cc, bc)
        w = scores[..., None] * decay * dtc[:, None]
        y_intra = jnp.einsum("btsgh,bsghp->btghp", w, xc)
        y_inter = jnp.einsum("btgn,bghpn->btghp", cc, state) * jnp.exp(cum)[..., None]
        w_state = jnp.exp(cum[:, -1:] - cum) * dtc
        new_state = (jnp.exp(cum[:, -1])[..., None, None] * state
                     + jnp.einsum("bsgh,bsghp,bsgn->bghpn", w_state, xc, bc))
        return new_state, y_intra + y_inter

    init = jnp.zeros((bsz, SSD_GROUPS, SSD_HPG, SSD_HEAD_DIM, SSD_STATE), jnp.float32)
    _, ys = lax.scan(step, init, xs)
    return from_chunks(ys)


def ssd_mixer(z, xbc, dt_raw, conv_w, conv_b, dt_bias, a_log, d_skip, norm_w):
    f32 = jnp.float32
    bsz, seqlen = z.shape[:2]
    xbc = jax.nn.silu(depthwise_conv_centred(xbc, conv_w) + conv_b.astype(xbc.dtype))
    xm, bm, cm = jnp.split(xbc, (SSD_WIDTH, SSD_WIDTH + SSD_GROUPS * SSD_STATE), axis=-1)
    xh = xm.astype(f32).reshape(bsz, seqlen, SSD_GROUPS, SSD_HPG, SSD_HEAD_DIM)
    bg = bm.astype(f32).reshape(bsz, seqlen, SSD_GROUPS, SSD_STATE)
    cg = cm.astype(f32).reshape(bsz, seqlen, SSD_GROUPS, SSD_STATE)
    y = xh * d_skip.astype(f32).reshape(SSD_GROUPS, SSD_HPG)[..., None]
    for d in range(2):
        dt = jax.nn.softplus(dt_raw.astype(f32) + dt_bias[d].astype(f32))
        dt = dt.reshape(bsz, seqlen, SSD_GROUPS, SSD_HPG)
        a_neg = -jnp.exp(a_log[d].astype(f32)).reshape(SSD_GROUPS, SSD_HPG)
        if d == 0:
            y = y + ssd_direction(xh, bg, cg, dt, a_neg)
        else:
            y = y + flip_seq(ssd_direction(flip_seq(xh), flip_seq(bg), flip_seq(cg), flip_seq(dt), a_neg))
    y = y.reshape(bsz, seqlen, SSD_WIDTH)
    return rmsnorm(y * jax.nn.silu(z.astype(f32)), norm_w)


def hgrn2_direction(q, log_f, k, v):
    bsz = q.shape[0]
    xs = tuple(to_chunks(t, HG_CHUNK) for t in (q, log_f, k, v))
    tri = jnp.tril(jnp.ones((HG_CHUNK, HG_CHUNK), dtype=bool))[None, :, :, None, None]

    def step(state, inp):
        qc, gc, kc, vc = inp
        cum = jnp.cumsum(gc, axis=1)
        seg = cum[:, :, None] - cum[:, None, :]
        decay = masked_decay(seg, tri)
        attn = jnp.einsum("bthk,btshk,bshk->btsh", qc, decay, kc)
        y_intra = jnp.einsum("btsh,bshv->bthv", attn, vc)
        y_inter = jnp.einsum("bthk,bhkv->bthv", qc * jnp.exp(cum), state)
        cum_end = cum[:, -1]
        new_state = (jnp.exp(cum_end)[..., None] * state
                     + jnp.einsum("bshk,bshv->bhkv", kc * jnp.exp(cum_end[:, None] - cum), vc))
        return new_state, y_intra + y_inter

    init = jnp.zeros((bsz, HG_HEADS, HG_DK, HG_DV), jnp.float32)
    _, ys = lax.scan(step, init, xs)
    return from_chunks(ys)


def hgrn2_mixer(q, f_fwd, f_bwd, i, g, lb, norm_w):
    f32 = jnp.float32
    bsz, seqlen = q.shape[:2]
    heads = lambda t: t.astype(f32).reshape(bsz, seqlen, HG_HEADS, -1)
    qh, vh = heads(q), heads(i)
    y = jnp.zeros((bsz, seqlen, HG_HEADS, HG_DV), f32)
    for d, f_raw in enumerate((f_fwd, f_bwd)):
        lbd = lb[d].reshape(HG_HEADS, HG_DK)
        fh = heads(f_raw)
        log_f = jnp.logaddexp(jnp.log(jnp.maximum(lbd, LB_FLOOR)),
                              jnp.log1p(-lbd) + jax.nn.log_sigmoid(fh))
        kh = (1.0 - lbd) * jax.nn.sigmoid(-fh)
        if d == 0:
            y = y + hgrn2_direction(qh, log_f, kh, vh)
        else:
            y = y + flip_seq(hgrn2_direction(flip_seq(qh), flip_seq(log_f), flip_seq(kh), flip_seq(vh)))
    y = rmsnorm(y, norm_w).reshape(bsz, seqlen, HG_WIDTH)
    return y * jax.nn.silu(g.astype(f32))


def math_log(v):
    return float(np.log(v))


def setup_inputs(seed: int = 0) -> dict:
    key = jax.random.key(seed)
    ks = jax.random.split(key, 20)
    nrm = jax.random.normal
    f32 = jnp.float32
    x = nrm(ks[0], (BATCH, SEQ, D_MODEL), f32)
    c = nrm(ks[1], (BATCH, D_MODEL), f32)
    ada_w = nrm(ks[2], (DEPTH, D_MODEL, N_MOD * D_MODEL), f32) * (0.5 * D_MODEL ** -0.5)
    ada_b = nrm(ks[3], (DEPTH, N_MOD * D_MODEL), f32) * 0.02
    norm_mix_w = 1.0 + 0.02 * nrm(ks[4], (DEPTH, D_MODEL), f32)
    norm_mlp_w = 1.0 + 0.02 * nrm(ks[5], (DEPTH, D_MODEL), f32)
    w_in = nrm(ks[6], (DEPTH, D_MODEL, D_IN_PROJ), f32) * D_MODEL ** -0.5
    conv_w = nrm(ks[7], (DEPTH, SSD_CONV, XBC_WIDTH), f32) * SSD_CONV ** -0.5
    conv_b = nrm(ks[8], (DEPTH, XBC_WIDTH), f32) * 0.02
    dt0 = jnp.exp(jax.random.uniform(ks[9], (DEPTH, 2, SSD_HEADS), f32,
                                     minval=math_log(1e-3), maxval=math_log(1e-1)))
    dt_bias = dt0 + jnp.log(-jnp.expm1(-dt0))
    a_log = jnp.log(jax.random.uniform(ks[10], (DEPTH, 2, SSD_HEADS), f32, minval=1.0, maxval=16.0))
    d_skip = 1.0 + 0.1 * nrm(ks[11], (DEPTH, SSD_HEADS), f32)
    ssd_norm_w = 1.0 + 0.02 * nrm(ks[12], (DEPTH, SSD_WIDTH), f32)
    hg_lower_bounds = 0.5 * nrm(ks[13], (2, DEPTH, HG_WIDTH), f32)
    hg_norm_w = 1.0 + 0.02 * nrm(ks[14], (DEPTH, HG_DV), f32)
    w_out = nrm(ks[15], (DEPTH, D_MIX, D_MODEL), f32) * D_MIX ** -0.5
    w_up = nrm(ks[16], (DEPTH, D_MODEL, D_FF), f32) * D_MODEL ** -0.5
    w_down = nrm(ks[17], (DEPTH, D_FF, D_MODEL), f32) * D_FF ** -0.5
    final_norm_w = 1.0 + 0.02 * nrm(ks[18], (D_MODEL,), f32)
    return {"x": x, "c": c, "ada_w": ada_w, "ada_b": ada_b, "norm_mix_w": norm_mix_w,
            "norm_mlp_w": norm_mlp_w, "w_in": w_in, "conv_w": conv_w, "conv_b": conv_b,
            "dt_bias": dt_bias, "a_log": a_log, "d_skip": d_skip, "ssd_norm_w": ssd_norm_w,
            "hg_lower_bounds": hg_lower_bounds, "hg_norm_w": hg_norm_w, "w_out": w_out,
            "w_up": w_up, "w_down": w_down, "final_norm_w": final_norm_w}


def reference(x, c, ada_w, ada_b, norm_mix_w, norm_mlp_w, w_in, conv_w, conv_b, dt_bias, a_log,
              d_skip, ssd_norm_w, hg_lower_bounds, hg_norm_w, w_out, w_up, w_down, final_norm_w):
    p = jax.nn.softmax(hg_lower_bounds.astype(jnp.float32), axis=1)
    lower_bounds = jnp.cumsum(p, axis=1) - p[:, :1]
    c_act = jax.nn.silu(c)
    for l in range(DEPTH):
        mod = c_act @ ada_w[l] + ada_b[l]
        shift1, scale1, gate1, shift2, scale2, gate2 = jnp.split(mod[:, None, :], N_MOD, axis=-1)
        h = rmsnorm(x, norm_mix_w[l]) * (1.0 + scale1) + shift1
        proj = h @ w_in[l]
        z, xbc, dt_raw, q, f_fwd, f_bwd, i, g = jnp.split(proj, IN_SPLITS, axis=-1)
        y_ssd = ssd_mixer(z, xbc, dt_raw, conv_w[l], conv_b[l], dt_bias[l], a_log[l],
                          d_skip[l], ssd_norm_w[l])
        y_hg = hgrn2_mixer(q, f_fwd, f_bwd, i, g, lower_bounds[:, l], hg_norm_w[l])
        y = jnp.concatenate([y_ssd, y_hg], axis=-1).astype(x.dtype) @ w_out[l]
        x = x + gate1 * y
        h = rmsnorm(x, norm_mlp_w[l]) * (1.0 + scale2) + shift2
        x = x + gate2 * (jnp.square(jax.nn.relu(h @ w_up[l])) @ w_down[l])
    return rmsnorm(x, final_norm_w)
```

```python
import numpy as np
import concourse.bass as bass
import concourse.mybir as mybir
from concourse.bass_utils import run_bass_kernel_spmd

F32 = mybir.dt.float32
BF16 = mybir.dt.bfloat16
AF = mybir.ActivationFunctionType
ALU = mybir.AluOpType

L = 4096
D = 1024
NT = 8
DEPTH = 4
EPS = 1e-6
C_Z, C_XBC, C_DT, C_Q, C_FF, C_FB, C_I, C_G = 0, 1024, 2560, 2576, 3600, 4624, 5648, 6672
K_ID, K_UF, K_UB, K_LF, K_LB, K_ONE, K_NEGF, K_NEGB, K_M64F, K_M64B, K_RM, K_SC = (
    0, 128, 256, 384, 512, 640, 768, 1024, 1280, 1408, 1536, 2048)
NCONST = 2056


def make_consts():
    r = np.arange(128)[:, None]
    c = np.arange(128)[None, :]
    k = np.zeros((128, NCONST), np.float32)
    k[:, K_ID:K_ID + 128] = (r == c)
    k[:, K_UF:K_UF + 128] = (r <= c)
    k[:, K_UB:K_UB + 128] = (r >= c)
    k[:, K_LF:K_LF + 128] = (r > c)
    k[:, K_LB:K_LB + 128] = (r < c)
    k[:, K_ONE:K_ONE + 128] = 1.0
    negf = np.where(r <= c, 0.0, -30000.0)
    negb = np.where(r >= c, 0.0, -30000.0)
    k[:, K_NEGF:K_NEGF + 256] = np.concatenate([negf, negf], 1)
    k[:, K_NEGB:K_NEGB + 256] = np.concatenate([negb, negb], 1)
    same = (r // 32) == (c // 32)
    k[:, K_M64F:K_M64F + 128] = same & (r <= c)
    k[:, K_M64B:K_M64B + 128] = same & (r >= c)
    rm = np.ones((128, 512), np.float32)
    rm[:, 0::32] = 0.0
    k[:, K_RM:K_RM + 512] = rm
    k[:, K_SC + 0] = 1.0
    k[:, K_SC + 1] = 0.0
    k[:, K_SC + 2] = D * EPS
    k[:, K_SC + 3] = 128 * EPS
    k[:, K_SC + 4] = ((np.arange(128) % 64) >= 32)
    return k


class T:
    __slots__ = ("ap", "w", "r", "name")

    def __init__(self, ap, name=""):
        self.ap = ap
        self.w = None
        self.r = {}
        self.name = name

    def __getitem__(self, k):
        return self.ap[k]


class Prog:
    ENG = ("pe", "act", "dve", "pool", "sp")

    def __init__(self, nc):
        self.nc = nc
        self.ops = {e: [] for e in self.ENG}
        self.cnt = {e: 0 for e in self.ENG}
        self.waited = {e: {} for e in self.ENG}
        self.semh = {}
        self.dtot = {}
        self.nins = 0

    def sem(self, key):
        if key not in self.semh:
            self.semh[key] = self.nc.alloc_semaphore(name="s%d" % len(self.semh))
        return self.semh[key]

    def _deps(self, eng, reads, writes):
        need = {}

        def add(ev, war):
            if ev is None:
                return
            key, val = ev
            if key == ("eng", eng) and eng == "pe":
                return
            if key[0] == "dma":
                val = self.dtot[key]
            if need.get(key, 0) < val:
                need[key] = val
        for t in reads:
            add(t.w, False)
        for t in writes:
            add(t.w, False)
            for ev in t.r.values():
                add(ev, True)
        out = []
        wd = self.waited[eng]
        for key, val in need.items():
            if wd.get(key, 0) >= val:
                continue
            wd[key] = val
            out.append((key, val))
        return out

    def op(self, eng, fn, reads=(), writes=()):
        waits = self._deps(eng, reads, writes)
        self.cnt[eng] += 1
        ev = (("eng", eng), self.cnt[eng])
        for t in reads:
            t.r[ev[0]] = ev
        for t in writes:
            t.w = ev
            t.r = {}
        self.ops[eng].append((waits, fn, True))
        self.nins += 1

    def dma(self, q, out, in_, reads, writes, semt):
        waits = self._deps(q, reads, writes)
        key = ("dma", id(semt))
        self.dtot[key] = self.dtot.get(key, 0) + 16
        ev = (key, self.dtot[key])
        for t in reads:
            t.r[key] = ev
        for t in writes:
            t.w = ev
            t.r = {}
        self.ops[q].append((waits, lambda e: e.dma_start(out=out, in_=in_), key))
        self.nins += 1

    def barrier(self):
        for e in self.ENG:
            waits = []
            for o in ("pe", "act", "dve", "pool"):
                if o == e or self.cnt[o] == 0:
                    continue
                key = ("eng", o)
                if self.waited[e].get(key, 0) < self.cnt[o]:
                    self.waited[e][key] = self.cnt[o]
                    waits.append((key, self.cnt[o]))
            for key, val in self.dtot.items():
                if self.waited[e].get(key, 0) < val:
                    self.waited[e][key] = val
                    waits.append((key, val))
            if waits:
                self.ops[e].append((waits, None, None))

    def emit(self):
        nc = self.nc
        with nc.Block() as block:
            def mk(name):
                def f(e):
                    mysem = self.sem(("eng", name))
                    for waits, fn, inc in self.ops[name]:
                        for key, val in waits:
                            e.wait_ge(self.sem(key), val)
                        if fn is None:
                            continue
                        ins = fn(e)
                        if inc is True:
                            ins.then_inc(mysem, 1)
                        else:
                            ins.then_inc(self.sem(inc), 16)
                return f
            block.tensor(mk("pe"))
            block.scalar(mk("act"))
            block.vector(mk("dve"))
            block.gpsimd(mk("pool"))
            block.sync(mk("sp"))

    def mm(self, out, lhsT, rhs, start, stop, R, W):
        self.op("pe", lambda e: e.matmul(out, lhsT=lhsT, rhs=rhs, start=start, stop=stop), R, W)

    def tr(self, out, in_, ident, R, W):
        self.op("pe", lambda e: e.transpose(out, in_, ident), R, W)

    def act(self, out, in_, func, R, W, bias=None, scale=None):
        kw = {}
        if bias is not None:
            kw["bias"] = bias
        if scale is not None:
            kw["scale"] = scale
        self.op("act", lambda e: e.activation(out, in_, func, **kw), R, W)

    def tt(self, out, in0, in1, op, R, W, eng="dve"):
        self.op(eng, lambda e: e.tensor_tensor(out, in0, in1, op), R, W)

    def ts(self, out, in0, s1, s2, op0, op1, R, W, eng="dve"):
        if op1 is None:
            self.op(eng, lambda e: e.tensor_scalar(out, in0, s1, None, op0), R, W)
        else:
            self.op(eng, lambda e: e.tensor_scalar(out, in0, s1, s2, op0, op1), R, W)

    def stt(self, out, in0, scalar, in1, op0, op1, R, W):
        self.op("dve", lambda e: e.scalar_tensor_tensor(out, in0, scalar, in1, op0, op1), R, W)

    def recip(self, out, in_, R, W):
        self.op("dve", lambda e: e.reciprocal(out, in_), R, W)

    def rsqrt(self, out, in_, bias_ap, R, W):
        self.act(out, in_, AF.Ln, R, W, bias=bias_ap)
        self.act(out, out, AF.Exp, W, W, scale=-0.5)

    def cp(self, out, in_, R, W, eng="dve"):
        self.op(eng, lambda e: e.tensor_copy(out, in_), R, W)

    def ms(self, ap, val, W, eng="dve"):
        self.op(eng, lambda e: e.memset(ap, val), (), W)

    def scan(self, out, d0, d1, op0, op1, R, W):
        self.op("dve", lambda e: e.tensor_tensor_scan(out, d0, d1, 0.0, op0, op1), R, W)


class Arena:
    def __init__(self, base):
        self.base = base
        self.off = 0
        self.cap = base.shape[1]

    def alloc(self, cols, dt=F32, name=""):
        n32 = cols if dt == F32 else (cols + 1) // 2
        assert self.off + n32 <= self.cap, "SBUF arena overflow at %s (%d + %d)" % (name, self.off, n32)
        a = self.base[:, self.off:self.off + n32]
        self.off += n32
        if dt != F32:
            a = a.bitcast(BF16)
        return T(a, name)


def build(depth=DEPTH, do_ssd=True, do_hg=True, dbg=False):
    nc = bass.Bass("TRN2", target_bir_lowering=False)
    P = Prog(nc)

    def din(name, shape):
        return nc.dram_tensor(name, list(shape), F32, kind="ExternalInput").ap()
    x_d = din("x", (L, D))
    c_d = din("c", (8, 128))
    adaw_d = din("ada_w", (DEPTH, D, 6 * D))
    vecA_d = din("vecA", (DEPTH, 73, 128))
    vecB_d = din("vecB", (DEPTH, 72, 128))
    vecC_d = din("vecC", (72, 128))
    dtb_d = din("dtb", (DEPTH, 32))
    alog_d = din("alog", (DEPTH, 32))
    dsk_d = din("dsk", (DEPTH, 2, 8))
    win_d = din("w_in", (DEPTH, D, 7696))
    wout_d = din("w_out", (DEPTH, 2 * D, D))
    wup_d = din("w_up", (DEPTH, D, 4 * D))
    wdn_d = din("w_down", (DEPTH, 4 * D, D))
    k_d = din("consts", (128, NCONST))
    out_d = nc.dram_tensor("out", [L, D], F32, kind="ExternalOutput").ap()
    xT_d = nc.dram_tensor("xT_s", [8, 128, L], F32, kind="Internal").ap()
    yb_d = nc.dram_tensor("yb_s", [16, 128, L], BF16, kind="Internal").ap()
    wupb_d = [nc.dram_tensor("wupb%d" % i, [D, 4 * D], BF16, kind="Internal").ap() for i in range(2)]
    wdnb_d = [nc.dram_tensor("wdnb%d" % i, [4 * D, D], BF16, kind="Internal").ap() for i in range(2)]
    wupT = [T(None, "wupb%d" % i) for i in range(2)]
    wdnT = [T(None, "wdnb%d" % i) for i in range(2)]

    def precast(l_):
        w_ = l_ % 2
        for r8 in range(8):
            P.dma("pool", wupb_d[w_][r8 * 128:(r8 + 1) * 128, :], wup_d[l_, r8 * 128:(r8 + 1) * 128, :],
                  [], [wupT[w_]], wupT[w_])
        for r8 in range(8):
            P.dma("pool", wdnb_d[w_][r8 * 512:(r8 + 1) * 512, :], wdn_d[l_, r8 * 512:(r8 + 1) * 512, :],
                  [], [wdnT[w_]], wdnT[w_])
    xTt = [T(None, "xT%d" % i) for i in range(NT)]
    ybt = [T(None, "yb%d" % i) for i in range(16)]
    outT = T(None, "out")

    ctxs = []
    cm = nc.sbuf_tensor("arena", [128, 53200], F32)
    sb = cm.__enter__()
    ctxs.append(cm)
    A = Arena(sb[:, :])
    PS = []
    for i in range(8):
        cm = nc.psum_tensor("ps%d" % i, [128, 512], F32)
        PS.append(T(cm.__enter__()[:, :], "ps%d" % i))
        ctxs.append(cm)
    class _NPS:
        def __init__(self):
            self.banks = [2, 3, 4, 5, 6, 7]
            self.i = 0

        def __call__(self):
            self.i = (self.i + 1) % len(self.banks)
            return PS[self.banks[self.i]]
    nps = _NPS()

    KC = A.alloc(NCONST, F32, "consts")
    P.dma("sp", KC.ap, k_d[:, :], [], [KC], KC)
    ID, UF, UB, LF, LB, ONE = (KC[:, o:o + 128] for o in (K_ID, K_UF, K_UB, K_LF, K_LB, K_ONE))
    NEG = (KC[:, K_NEGF:K_NEGF + 256], KC[:, K_NEGB:K_NEGB + 256])
    M64 = (KC[:, K_M64F:K_M64F + 128], KC[:, K_M64B:K_M64B + 128])
    RM = KC[:, K_RM:K_RM + 512]
    c_one = KC[:, K_SC:K_SC + 1]
    c_deps = KC[:, K_SC + 2:K_SC + 3]
    c_heps = KC[:, K_SC + 3:K_SC + 4]
    c_rmask = KC[:, K_SC + 4:K_SC + 5]
    IDB = A.alloc(128, BF16, "idb")
    P.cp(IDB.ap, ID, [KC], [IDB])
    Ulist = (UF, UB)
    Llist = (LF, LB)

    colA = [A.alloc(73, F32, "colA%d" % l) for l in range(DEPTH)]
    colB = [A.alloc(72, F32, "colB%d" % l) for l in range(DEPTH)]
    colC = A.alloc(72, F32, "colC")
    modT = [A.alloc(48, F32, "mod%d" % l) for l in range(DEPTH)]
    wm1 = [A.alloc(8, F32) for l in range(DEPTH)]
    wm2 = [A.alloc(8, F32) for l in range(DEPTH)]
    negcb = [A.alloc(12, F32) for l in range(DEPTH)]
    LBt = A.alloc(64, F32, "lb")
    LN1 = A.alloc(64, F32, "ln1mlb")
    fnw = A.alloc(8, F32, "fnw")
    dtbb = [A.alloc(32, F32) for l in range(DEPTH)]
    anb = [A.alloc(32, F32) for l in range(DEPTH)]
    dcol = [A.alloc(8, F32) for l in range(DEPTH)]
    cact = A.alloc(16, F32, "cact")
    mark0 = A.off

    stg = A.alloc(128, F32, "stg")
    tmpc = A.alloc(128, F32, "tmpc")

    def load_cols(src_ap, rows, dst):
        P.dma("sp", stg[0:rows, :], src_ap, [], [stg], stg)
        ps = nps()
        P.tr(ps[:, 0:rows], stg[0:rows, :], ID[0:rows, 0:rows], [stg, KC], [ps])
        P.cp(dst[:, 0:rows], ps[:, 0:rows], [ps], [dst])
    for l in range(DEPTH):
        load_cols(vecA_d[l], 73, colA[l])
        load_cols(vecB_d[l], 72, colB[l])
        P.ts(negcb[l].ap, colB[l][:, 60:72], -1.0, None, ALU.mult, None, [colB[l]], [negcb[l]])
        P.dma("sp", dtbb[l].ap, dtb_d[l].partition_broadcast(128), [], [dtbb[l]], dtbb[l])
        P.dma("sp", anb[l].ap, alog_d[l].partition_broadcast(128), [], [anb[l]], anb[l])
        P.act(anb[l].ap, anb[l].ap, AF.Exp, [anb[l]], [anb[l]])
        P.ts(anb[l].ap, anb[l].ap, -1.0, None, ALU.mult, None, [anb[l]], [anb[l]])
        P.dma("sp", dcol[l][0:64, :], dsk_d[l, 0].partition_broadcast(64), [], [dcol[l]], dcol[l])
        P.dma("sp", dcol[l][64:128, :], dsk_d[l, 1].partition_broadcast(64), [], [dcol[l]], dcol[l])
    load_cols(vecC_d, 72, colC)
    P.cp(fnw.ap, colC[:, 64:72], [colC], [fnw])
    P.ts(fnw.ap, fnw.ap, 32.0, None, ALU.mult, None, [fnw], [fnw])
    load_cols(c_d, 8, tmpc)
    ctmp = A.alloc(16, F32)
    P.act(ctmp[:, 0:8], tmpc[:, 0:8], AF.Exp, [tmpc], [ctmp], scale=-1.0)
    P.ts(ctmp[:, 0:8], ctmp[:, 0:8], 1.0, None, ALU.add, None, [ctmp], [ctmp])
    P.recip(ctmp[:, 0:8], ctmp[:, 0:8], [ctmp], [ctmp])
    cv = cact.ap.rearrange("p (k t) -> p k t", t=2)
    P.tt(cv[:, :, 0], tmpc[:, 0:8], ctmp[:, 0:8], ALU.mult, [tmpc, ctmp], [cact])
    P.tt(cv[:, :, 1], tmpc[:, 0:8], ctmp[:, 0:8], ALU.mult, [tmpc, ctmp], [cact])
    ev_ = A.alloc(64, F32)
    sm_ = A.alloc(16, F32)
    P.act(ev_.ap, colC[:, 0:64], AF.Exp, [colC], [ev_])
    e4 = ev_.ap.rearrange("p (d l h) -> p d l h", d=2, l=4)
    s3 = sm_.ap.rearrange("p (d h) -> p d h", d=2)
    P.tt(s3, e4[:, :, 0, :], e4[:, :, 1, :], ALU.add, [ev_], [sm_])
    P.tt(s3, s3, e4[:, :, 2, :], ALU.add, [ev_, sm_], [sm_])
    P.tt(s3, s3, e4[:, :, 3, :], ALU.add, [ev_, sm_], [sm_])
    P.recip(sm_.ap, sm_.ap, [sm_], [sm_])
    lb4 = LBt.ap.rearrange("p (d l h) -> p d l h", d=2, l=4)
    P.ms(LBt.ap, 0.0, [LBt])
    for l in range(1, DEPTH):
        P.tt(ev_.ap.rearrange("p (d l h) -> p d l h", d=2, l=4)[:, :, l, :], e4[:, :, l, :], s3, ALU.mult, [ev_, sm_], [ev_])
        P.tt(lb4[:, :, l, :], lb4[:, :, l - 1, :], e4[:, :, l, :], ALU.add, [ev_, LBt], [LBt])
    P.act(LN1.ap, LBt.ap, AF.Ln, [LBt, KC], [LN1], bias=c_one, scale=-1.0)
    awb = [A.alloc(8 * 512, F32, "awb%d" % i) for i in range(2)]
    mrow = A.alloc(6144, F32, "mrow")
    for l in range(DEPTH):
        for pc in range(12):
            wb = awb[pc % 2]
            wv = wb.ap.rearrange("p (k n) -> p k n", k=8)
            P.dma("sp", wv, adaw_d[l, :, pc * 512:(pc + 1) * 512].rearrange("(k p) n -> p k n", p=128), [], [wb], wb)
            psr = nps()
            for kc in range(8):
                P.mm(psr[0:2, :], cv[:, kc, :], wv[:, kc, :], kc == 0, kc == 7, [wb, cact], [psr])
            P.act(mrow[0:2, pc * 512:(pc + 1) * 512], psr[0:2, :], AF.Copy, [psr], [mrow])
        psm = nps()
        for j in range(48):
            P.tr(psm[:, 2 * j:2 * j + 2], mrow[0:2, j * 128:(j + 1) * 128], ID[0:2, 0:2], [mrow, KC], [psm])
        pv = psm[:, 0:96].rearrange("p (c t) -> p c t", t=2)
        P.tt(modT[l].ap, pv[:, :, 0], colA[l][:, 0:48], ALU.add, [psm, colA[l]], [modT[l]])
        P.stt(wm1[l].ap, modT[l][:, 8:16], 1.0, colA[l][:, 48:56], ALU.add, ALU.mult, [modT[l], colA[l]], [wm1[l]])
        P.ts(wm1[l].ap, wm1[l].ap, 32.0, None, ALU.mult, None, [wm1[l]], [wm1[l]])
        P.stt(wm2[l].ap, modT[l][:, 32:40], 1.0, colA[l][:, 56:64], ALU.add, ALU.mult, [modT[l], colA[l]], [wm2[l]])
        P.ts(wm2[l].ap, wm2[l].ap, 32.0, None, ALU.mult, None, [wm2[l]], [wm2[l]])
    precast(0)
    xin = [A.alloc(1024, F32, "xin%d" % i) for i in range(2)]
    xst = [A.alloc(8 * 512, F32, "xst%d" % i) for i in range(2)]
    for tt in range(NT):
        st = xst[tt % 2]
        sv = st.ap.rearrange("p (k t) -> p k t", k=8)
        for s in range(4):
            i = tt * 4 + s
            xi = xin[i % 2]
            P.dma("sp", xi.ap, x_d[i * 128:(i + 1) * 128, :], [], [xi], xi)
            for half in range(2):
                ps = nps()
                for q in range(4):
                    kc = half * 4 + q
                    P.tr(ps[:, q * 128:(q + 1) * 128], xi[:, kc * 128:(kc + 1) * 128], ID, [xi, KC], [ps])
                P.op("act", (lambda o, i_: (lambda e: e.activation(o, i_, AF.Copy)))(
                    sv[:, half * 4:half * 4 + 4, s * 128:(s + 1) * 128],
                    ps[:, :].rearrange("p (k t) -> p k t", k=4)), [ps], [st])
        P.dma("sp", xT_d[:, :, tt * 512:(tt + 1) * 512].rearrange("k p t -> p k t"), sv, [st], [xTt[tt]], st)
    P.barrier()
    A.off = mark0

    ssq = A.alloc(L, F32, "ssq")
    mark_h = A.off
    hT = A.alloc(8 * L, BF16, "hT")
    hv = hT.ap.rearrange("p (k t) -> p k t", k=8)
    hTt = [T(None, "hT%d" % i) for i in range(NT)]
    mark1 = A.off

    def norm_tile(xs, sq, rs, wmod, shift, dst_fn, R_extra, Wt):
        xv = xs.ap.rearrange("p (k t) -> p k t", k=8)
        qv = sq.ap.rearrange("p (k t) -> p k t", k=8)
        P.act(sq.ap, xs.ap, AF.Square, [xs], [sq])
        ps = nps()
        for kc in range(8):
            P.mm(ps[:, :], ONE, qv[:, kc, :], kc == 0, kc == 7, [sq, KC], [ps])
        P.rsqrt(rs.ap, ps[:, :], c_deps, [ps, KC], [rs])
        P.tt(qv, xv, rs.ap.unsqueeze(1).to_broadcast([128, 8, 512]), ALU.mult, [xs, rs], [sq])
        for kc in range(8):
            if shift is None:
                P.act(dst_fn(kc), qv[:, kc, :], AF.Identity, [sq] + R_extra, [Wt], scale=wmod[:, kc:kc + 1])
            else:
                P.act(dst_fn(kc), qv[:, kc, :], AF.Identity, [sq] + R_extra, [Wt],
                      bias=shift[:, kc:kc + 1], scale=wmod[:, kc:kc + 1])

    def load_w(dst, src3, q="pool"):
        P.dma(q, dst_ap_of(dst), src3, [], [dst], dst)

    def dst_ap_of(t):
        return t.ap

    def inproj(wt, ncols, tt, ps, col0=0):
        wv = wt.ap.rearrange("p (k n) -> p k n", k=8)
        for kc in range(8):
            P.mm(ps[:, :], wv[:, kc, col0:col0 + 128], hv[:, kc, tt * 512:(tt + 1) * 512],
                 kc == 0, kc == 7, [wt, hTt[tt]], [ps])

    def silu_from_ps(ps, dst_ap, Wt, e_t, bias_col=None, negbias_col=None, R_extra=()):
        R_extra = list(R_extra)
        if bias_col is None:
            P.act(e_t.ap, ps[:, :], AF.Exp, [ps], [e_t], scale=-1.0)
            P.act(e_t.ap, e_t.ap, AF.Ln, [e_t, KC], [e_t], bias=c_one)
            P.act(e_t.ap, e_t.ap, AF.Exp, [e_t], [e_t], scale=-1.0)
            P.tt(dst_ap, ps[:, :], e_t.ap, ALU.mult, [ps, e_t], [Wt])
        else:
            P.act(e_t.ap, ps[:, :], AF.Exp, [ps] + R_extra, [e_t], bias=negbias_col, scale=-1.0)
            P.act(e_t.ap, e_t.ap, AF.Ln, [e_t, KC], [e_t], bias=c_one)
            P.act(e_t.ap, e_t.ap, AF.Exp, [e_t], [e_t], scale=-1.0)
            P.stt(dst_ap, ps[:, :], bias_col, e_t.ap, ALU.add, ALU.mult, [ps, e_t] + R_extra, [Wt])

    for l in range(depth):
        A.off = mark1
        xs2 = [A.alloc(8 * 512, F32, "xs%d" % i) for i in range(2)]
        sq = A.alloc(8 * 512, F32, "sq")
        rs = A.alloc(512, F32, "rs")
        for tt in range(NT):
            xs = xs2[tt % 2]
            P.dma("sp", xs.ap.rearrange("p (k t) -> p k t", k=8),
                  xT_d[:, :, tt * 512:(tt + 1) * 512].rearrange("k p t -> p k t"), [xTt[tt]], [xs], xs)
            norm_tile(xs, sq, rs, wm1[l], modT[l][:, 0:8],
                      lambda kc, tt=tt: hv[:, kc, tt * 512:(tt + 1) * 512], [wm1[l], modT[l]], hTt[tt])
        P.barrier()
        A.off = mark1

        if do_ssd:
            ssd_phase(P, A, nps, l, locals())
        else:
            P.ms(ssq.ap, 1.0, [ssq])
            zt = A.alloc(512, BF16)
            P.ms(zt.ap, 0.0, [zt])
            for b in range(8):
                for tt in range(NT):
                    P.dma("sp", yb_d[b, :, tt * 512:(tt + 1) * 512], zt.ap, [zt], [ybt[b]], zt)
        P.barrier()
        A.off = mark1
        if do_hg:
            hg_phase(P, A, nps, l, locals())
        else:
            zt = A.alloc(512, BF16)
            P.ms(zt.ap, 0.0, [zt])
            for b in range(8, 16):
                for tt in range(NT):
                    P.dma("sp", yb_d[b, :, tt * 512:(tt + 1) * 512], zt.ap, [zt], [ybt[b]], zt)
        P.barrier()
        A.off = mark_h

        wsel = l % 2
        wo = A.alloc(16 * 1024, BF16, "wo")
        P.dma("pool", wo.ap.rearrange("p (k n) -> p k n", k=16),
              wout_d[l].rearrange("(k p) n -> p k n", p=128), [], [wo], wo)
        if l + 1 < depth:
            precast(l + 1)
        wov = wo.ap.rearrange("p (k n) -> p k n", k=16)
        rss = ssq
        P.rsqrt(rss.ap, ssq.ap, c_deps, [ssq, KC], [rss])
        ych = A.alloc(16 * 512, BF16, "ych")
        yv = ych.ap.rearrange("p (k t) -> p k t", k=16)
        xs2 = [A.alloc(8 * 512, F32, "xsb%d" % i) for i in range(2)]
        sq = A.alloc(8 * 512, F32, "sqb")
        rs = A.alloc(512, F32, "rsb")
        h2 = A.alloc(8 * 512, BF16, "h2")
        h2v = h2.ap.rearrange("p (k t) -> p k t", k=8)
        up = A.alloc(32 * 512, BF16, "up")
        upv = up.ap.rearrange("p (k t) -> p k t", k=32)
        ring = [A.alloc(4096, BF16, "ring%d" % i) for i in range(4)]
        t1 = A.alloc(512, F32, "t1")
        rl = [A.alloc(512, F32, "rl%d" % i) for i in range(2)]
        ri = 0
        for tt in range(NT):
            xs = xs2[tt % 2]
            xv = xs.ap.rearrange("p (k t) -> p k t", k=8)
            P.dma("sp", xv, xT_d[:, :, tt * 512:(tt + 1) * 512].rearrange("k p t -> p k t"), [xTt[tt]], [xs], xs)
            P.dma("sp", yv, yb_d[:, :, tt * 512:(tt + 1) * 512].rearrange("k p t -> p k t"), ybt, [ych], ych)
            for db in range(8):
                ps1 = nps()
                for kc in range(8):
                    P.mm(ps1[:, :], wov[:, kc, db * 128:(db + 1) * 128], yv[:, kc, :], kc == 0, kc == 7, [wo, ych], [ps1])
                ps2 = nps()
                for kc in range(8, 16):
                    P.mm(ps2[:, :], wov[:, kc, db * 128:(db + 1) * 128], yv[:, kc, :], kc == 8, kc == 15, [wo, ych], [ps2])
                P.tt(t1.ap, ps1[:, :], rss[:, tt * 512:(tt + 1) * 512], ALU.mult, [ps1, rss], [t1])
                P.stt(t1.ap, t1.ap, 32.0, ps2[:, :], ALU.mult, ALU.add, [t1, ps2], [t1])
                P.stt(xv[:, db, :], t1.ap, modT[l][:, 16 + db:17 + db], xv[:, db, :], ALU.mult, ALU.add,
                      [t1, modT[l], xs], [xs])
            norm_tile(xs, sq, rs, wm2[l], modT[l][:, 24:32], lambda kc: h2v[:, kc, :], [wm2[l], modT[l]], h2)
            for fc in range(8):
                wu = ring[ri % 4]
                ri += 1
                wuv = wu.ap.rearrange("p (k n) -> p k n", k=8)
                P.dma("sp", wuv, wupb_d[wsel][:, fc * 512:(fc + 1) * 512].rearrange("(k p) n -> p k n", p=128),
                      [wupT[wsel]], [wu], wu)
                for j in range(4):
                    ps = nps()
                    for kc in range(8):
                        P.mm(ps[:, :], wuv[:, kc, j * 128:(j + 1) * 128], h2v[:, kc, :], kc == 0, kc == 7, [wu, h2], [ps])
                    r_ = rl[(fc * 4 + j) % 2]
                    P.act(r_.ap, ps[:, :], AF.Relu, [ps], [r_])
                    P.tt(upv[:, fc * 4 + j, :], r_.ap, r_.ap, ALU.mult, [r_], [up])
            for db in range(8):
                wd = ring[ri % 4]
                ri += 1
                wdv = wd.ap.rearrange("p (k n) -> p k n", k=32)
                P.dma("sp", wdv, wdnb_d[wsel][:, db * 128:(db + 1) * 128].rearrange("(k p) n -> p k n", p=128),
                      [wdnT[wsel]], [wd], wd)
                ps = nps()
                for fc in range(32):
                    P.mm(ps[:, :], wdv[:, fc, :], upv[:, fc, :], fc == 0, fc == 31, [wd, up], [ps])
                P.stt(xv[:, db, :], ps[:, :], modT[l][:, 40 + db:41 + db], xv[:, db, :], ALU.mult, ALU.add,
                      [ps, modT[l], xs], [xs])
            P.dma("sp", xT_d[:, :, tt * 512:(tt + 1) * 512].rearrange("k p t -> p k t"), xv, [xs], [xTt[tt]], xs)
        P.barrier()

    A.off = mark_h
    xs2 = [A.alloc(8 * 512, F32, "xsf%d" % i) for i in range(2)]
    sq = A.alloc(8 * 512, F32, "sqf")
    rs = A.alloc(512, F32, "rsf")
    of = A.alloc(8 * 512, F32, "of")
    ofv = of.ap.rearrange("p (k t) -> p k t", k=8)
    ot = [A.alloc(1024, F32, "ot%d" % i) for i in range(2)]
    for tt in range(NT):
        xs = xs2[tt % 2]
        P.dma("sp", xs.ap.rearrange("p (k t) -> p k t", k=8),
              xT_d[:, :, tt * 512:(tt + 1) * 512].rearrange("k p t -> p k t"), [xTt[tt]], [xs], xs)
        norm_tile(xs, sq, rs, fnw, None, lambda kc: ofv[:, kc, :], [fnw], of)
        for s in range(4):
            o_ = ot[s % 2]
            for half in range(2):
                ps = nps()
                for q in range(4):
                    kc = half * 4 + q
                    P.tr(ps[:, q * 128:(q + 1) * 128], ofv[:, kc, s * 128:(s + 1) * 128], ID, [of, KC], [ps])
                P.act(o_[:, half * 512:(half + 1) * 512], ps[:, :], AF.Copy, [ps], [o_])
            r0 = tt * 512 + s * 128
            P.dma("sp", out_d[r0:r0 + 128, :], o_.ap, [o_], [outT], o_)
    P.barrier()
    P.emit()
    for cm in reversed(ctxs):
        cm.__exit__(None, None, None)
    return nc, P


def conv_block(P, A, nps, env, l, wt, cb, dst, raw, dg, e_t):
    KC, ID = env["KC"], env["ID"]
    colB, negcb = env["colB"], env["negcb"]
    dgv = dg.ap.rearrange("p (j n) -> p j n", j=5)
    for j in range(5):
        P.ts(dgv[:, j, :], ID, colB[l][:, j * 12 + cb:j * 12 + cb + 1], None, ALU.mult, None, [KC, colB[l]], [dg])
    for tt in range(NT):
        ps = nps()
        env["inproj"](wt, 128, tt, ps)
        P.act(raw[:, 2 + tt * 512:2 + (tt + 1) * 512], ps[:, :], AF.Copy, [ps], [raw])
    for tt in range(NT):
        ps = nps()
        for j in range(5):
            P.mm(ps[:, :], dgv[:, j, :], raw[:, tt * 512 + j:tt * 512 + j + 512], j == 0, j == 4, [dg, raw], [ps])
        env["silu_from_ps"](ps, dst[:, tt * 512:(tt + 1) * 512], dst, e_t,
                            bias_col=colB[l][:, 60 + cb:61 + cb], negbias_col=negcb[l][:, cb:cb + 1],
                            R_extra=[colB[l], negcb[l]])


def ssd_phase(P, A, nps, l, env):
    KC, ID, IDB, ONE = env["KC"], env["ID"], env["IDB"], env["ONE"]
    hv, hTt, win_d, yb_d, ybt = env["hv"], env["hTt"], env["win_d"], env["yb_d"], env["ybt"]
    Ulist, Llist, NEG = env["Ulist"], env["Llist"], env["NEG"]
    ssq, colA, dcol, dtbb, anb = env["ssq"], env["colA"], env["dcol"], env["dtbb"], env["anb"]
    c_one = env["c_one"]
    inproj, silu_from_ps = env["inproj"], env["silu_from_ps"]

    def wload(wt, col0, n=128):
        P.dma("pool", wt.ap.rearrange("p (k n) -> p k n", k=8),
              win_d[l, :, col0:col0 + n].rearrange("(k p) n -> p k n", p=128), [], [wt], wt)
    wdt = A.alloc(8 * 16, BF16, "wdt")
    wload(wdt, C_DT, 16)
    wdv = wdt.ap.rearrange("p (k n) -> p k n", k=8)
    dt = [A.alloc(512, F32, "dt%d" % d) for d in range(2)]
    la = [A.alloc(512, F32, "la%d" % d) for d in range(2)]
    psd = nps()
    for i in range(32):
        for kc in range(8):
            P.mm(psd[:, i * 16:(i + 1) * 16], hv[:, kc, i * 128:(i + 1) * 128], wdv[:, kc, :], kc == 0, kc == 7,
                 [wdt, hTt[i // 4]], [psd])
    for d in range(2):
        d3 = dt[d].ap.rearrange("p (i h) -> p i h", h=16)
        l3 = la[d].ap.rearrange("p (i h) -> p i h", h=16)
        P.tt(d3, psd[:, :].rearrange("p (i h) -> p i h", h=16),
             dtbb[l][:, d * 16:(d + 1) * 16].unsqueeze(1).to_broadcast([128, 32, 16]), ALU.add, [psd, dtbb[l]], [dt[d]])
        P.act(dt[d].ap, dt[d].ap, AF.Exp, [dt[d]], [dt[d]])
        P.act(dt[d].ap, dt[d].ap, AF.Ln, [dt[d], KC], [dt[d]], bias=c_one)
        P.tt(l3, d3, anb[l][:, d * 16:(d + 1) * 16].unsqueeze(1).to_broadcast([128, 32, 16]), ALU.mult, [dt[d], anb[l]], [la[d]])
    CT = A.alloc(L, BF16, "CT")
    Btok = A.alloc(L, BF16, "Btok")
    Sg = A.alloc(L, BF16, "Sg")
    raw = A.alloc(L + 8, BF16, "raw")
    dg = A.alloc(5 * 128, BF16, "dg")
    e_t = A.alloc(512, F32, "e_t")
    xTb = A.alloc(L, BF16, "xTb")
    xtok = A.alloc(L, BF16, "xtok")
    yd = A.alloc(L, F32, "yd")
    BT = T(yd.ap[:, 0:2048].bitcast(BF16), "BT")
    st = [A.alloc(128, F32, "st%d" % d) for d in range(2)]
    stb = [A.alloc(128, BF16, "stb%d" % d) for d in range(2)]
    Ap = [[(A.alloc(256, BF16, "Aph"), A.alloc(256, BF16, "Apl")) for _ in range(2)] for d in range(2)]
    lah = [A.alloc(512, BF16, "lah%d" % d) for d in range(2)]
    lal = [A.alloc(512, BF16, "lal%d" % d) for d in range(2)]
    ltmp = e_t
    UB16 = [A.alloc(128, BF16, "ub16") for d in range(2)]
    LB16 = [A.alloc(128, BF16, "lb16") for d in range(2)]
    ONEB = A.alloc(128, BF16, "oneb")
    P.cp(ONEB.ap, ONE, [KC], [ONEB])
    for d in range(2):
        P.cp(UB16[d].ap, Ulist[d], [KC], [UB16[d]])
        P.cp(LB16[d].ap, Llist[d], [KC], [LB16[d]])
        P.cp(lah[d].ap, la[d].ap, [la[d]], [lah[d]])
        P.tt(ltmp.ap, la[d].ap, lah[d].ap, ALU.subtract, [la[d], lah[d]], [ltmp])
        P.cp(lal[d].ap, ltmp.ap, [ltmp], [lal[d]])
    EW = [[A.alloc(512, F32, "EW") for _ in range(2)] for d in range(2)]
    Es = [[A.alloc(4, F32, "Es") for _ in range(2)] for d in range(2)]
    Wm = [[A.alloc(256, BF16, "Wm") for _ in range(2)] for d in range(2)]
    CTs = [[A.alloc(256, BF16, "CTs") for _ in range(2)] for d in range(2)]
    xdt = [[A.alloc(128, BF16, "xdt") for _ in range(2)] for d in range(2)]
    xw = [[A.alloc(128, BF16, "xw") for _ in range(2)] for d in range(2)]
    stmp = [A.alloc(128, F32, "stmp%d" % d) for d in range(2)]
    NEGB16 = [A.alloc(256, BF16, "negb16_%d" % d) for d in range(2)]
    for d in range(2):
        P.cp(NEGB16[d].ap, NEG[d], [KC], [NEGB16[d]])
    psUfix = [[env["PS"][2], env["PS"][3]], [env["PS"][4], env["PS"][5]]]
    fin = A.alloc(512, F32, "fin")
    fin2 = A.alloc(512, F32, "fin2")
    ost = [A.alloc(512, BF16, "ost%d" % i) for i in range(2)]
    wx = A.alloc(8 * 128, BF16, "wx")
    wz = A.alloc(8 * 128, BF16, "wz")
    P.ms(raw[:, 0:2], 0.0, [raw])
    P.ms(raw[:, L + 2:L + 4], 0.0, [raw])
    psYb = [env["PS"][0], env["PS"][1]]
    for g in range(2):
        P.barrier()
        for which, dst in ((0, BT), (1, CT)):
            cb = 8 + which * 2 + g
            wload(wx, C_XBC + cb * 128, 128)
            conv_block(P, A, nps, env, l, wx, cb, dst, raw, dg, e_t)
        for i8 in range(4):
            ps = nps()
            pb = ps.ap.bitcast(BF16)
            for q in range(8):
                i = i8 * 8 + q
                P.tr(pb[:, q * 128:(q + 1) * 128], BT[:, i * 128:(i + 1) * 128], IDB.ap, [BT, IDB], [ps])
            P.act(Btok[:, i8 * 1024:(i8 + 1) * 1024], pb[:, 0:1024], AF.Copy, [ps], [Btok])
        for i4 in range(8):
            ps = nps()
            for q in range(4):
                i = i4 * 4 + q
                P.mm(ps[:, q * 128:(q + 1) * 128], BT[:, i * 128:(i + 1) * 128], CT[:, i * 128:(i + 1) * 128],
                     True, True, [BT, CT], [ps])
            P.act(Sg[:, i4 * 512:(i4 + 1) * 512], ps[:, :], AF.Copy, [ps], [Sg])
        P.barrier()
        for b in range(4 * g, 4 * g + 4):
            wload(wx, C_XBC + b * 128, 128)
            wload(wz, C_Z + b * 128, 128)
            conv_block(P, A, nps, env, l, wx, b, xTb, raw, dg, e_t)
            for i8 in range(4):
                ps = nps()
                pb = ps.ap.bitcast(BF16)
                for q in range(8):
                    i = i8 * 8 + q
                    P.tr(pb[:, q * 128:(q + 1) * 128], xTb[:, i * 128:(i + 1) * 128], IDB.ap, [xTb, IDB], [ps])
                P.act(xtok[:, i8 * 1024:(i8 + 1) * 1024], pb[:, 0:1024], AF.Copy, [ps], [xtok])
            xk = xtok.ap.rearrange("p (i h q) -> p i h q", h=2, q=64)
            psUs = {}

            def pre1(step, d):
                i = step if d == 0 else 31 - step
                bsel = step % 2
                (Aph_, Apl_), EW_ = Ap[d][bsel], EW[d][bsel]
                lahv = lah[d].ap.rearrange("p (i h) -> p i h", h=16)[:, i, 2 * b:2 * b + 2]
                lalv = lal[d].ap.rearrange("p (i h) -> p i h", h=16)[:, i, 2 * b:2 * b + 2]
                ubc = UB16[d].ap.unsqueeze(1).to_broadcast([128, 2, 128])
                P.tt(Aph_.ap.rearrange("p (h t) -> p h t", h=2), lahv.unsqueeze(2).to_broadcast([128, 2, 128]), ubc,
                     ALU.mult, [lah[d], UB16[d]], [Aph_])
                P.tt(Apl_.ap.rearrange("p (h t) -> p h t", h=2), lalv.unsqueeze(2).to_broadcast([128, 2, 128]), ubc,
                     ALU.mult, [lal[d], UB16[d]], [Apl_])
                psA = nps()
                P.mm(psA[:, 0:256], LB16[d].ap, Aph_.ap, True, False, [LB16[d], Aph_], [psA])
                P.mm(psA[:, 0:256], LB16[d].ap, Apl_.ap, False, False, [LB16[d], Apl_], [psA])
                P.mm(psA[:, 0:256], IDB.ap, NEGB16[d].ap, False, True, [IDB, NEGB16[d]], [psA])
                P.mm(psA[:, 256:512], ONEB.ap, Aph_.ap, True, False, [ONEB, Aph_], [psA])
                P.mm(psA[:, 256:512], ONEB.ap, Apl_.ap, False, True, [ONEB, Apl_], [psA])
                P.act(EW_.ap, psA[:, :], AF.Exp, [psA], [EW_])

            def pre2(step, d):
                i = step if d == 0 else 31 - step
                first = step == 0
                bsel = step % 2
                dtv = dt[d].ap.rearrange("p (i h) -> p i h", h=16)[:, i, 2 * b:2 * b + 2]
                EW_, Es_, Wm_, CTs_, xdt_, xw_ = (EW[d][bsel], Es[d][bsel], Wm[d][bsel],
                                                  CTs[d][bsel], xdt[d][bsel], xw[d][bsel])
                ew3 = EW_.ap.rearrange("p (a h t) -> p a h t", a=2, h=2)
                tcol = 127 if d == 0 else 0
                P.cp(Es_.ap.rearrange("p (a h) -> p a h", a=2), ew3[:, :, :, tcol], [EW_], [Es_], eng="pool")
                wm3 = Wm_.ap.rearrange("p (h t) -> p h t", h=2)
                ct3 = CTs_.ap.rearrange("p (h t) -> p h t", h=2)
                P.tt(wm3, ew3[:, 0], Sg[:, i * 128:(i + 1) * 128].unsqueeze(1).to_broadcast([128, 2, 128]),
                     ALU.mult, [EW_, Sg], [Wm_])
                if not first:
                    P.tt(ct3, ew3[:, 1], CT[:, i * 128:(i + 1) * 128].unsqueeze(1).to_broadcast([128, 2, 128]),
                         ALU.mult, [EW_, CT], [CTs_])
                xd3 = xdt_.ap.rearrange("p (h q) -> p h q", h=2)
                xw3 = xw_.ap.rearrange("p (h q) -> p h q", h=2)
                P.tt(xd3, xk[:, i], dtv.unsqueeze(2).to_broadcast([128, 2, 64]), ALU.mult, [xtok, dt[d]], [xdt_], eng="pool")
                P.tt(xw3, xd3, Es_[:, 0:2].unsqueeze(2).to_broadcast([128, 2, 64]), ALU.mult, [xdt_, Es_], [xw_], eng="pool")
                psU = psUfix[d][bsel]
                P.mm(psU[:, 0:128], Btok[:, i * 128:(i + 1) * 128], xw_.ap, True, True, [Btok, xw_], [psU])
                psUs[(step, d)] = psU

            def post(step, d):
                i = step if d == 0 else 31 - step
                first = step == 0
                bsel = step % 2
                Es_, Wm_, CTs_, xdt_ = Es[d][bsel], Wm[d][bsel], CTs[d][bsel], xdt[d][bsel]
                wm3 = Wm_.ap.rearrange("p (h t) -> p h t", h=2)
                ct3 = CTs_.ap.rearrange("p (h t) -> p h t", h=2)
                xd3 = xdt_.ap.rearrange("p (h q) -> p h q", h=2)
                psU = psUs.pop((step, d))
                tix = i // 4
                psY = psYb[d]
                slot = i % 4
                for hh in range(2):
                    o_ = psY[hh * 64:(hh + 1) * 64, slot * 128:(slot + 1) * 128]
                    P.mm(o_, xd3[:, hh, :], wm3[:, hh, :], True, first, [xdt_, Wm_], [psY])
                    if not first:
                        P.mm(o_, stb[d][:, hh * 64:(hh + 1) * 64], ct3[:, hh, :], False, True, [stb[d], CTs_], [psY])
                if first:
                    P.cp(st[d].ap, psU[:, 0:128], [psU], [st[d]])
                else:
                    s3_ = st[d].ap.rearrange("p (h q) -> p h q", h=2)
                    t3_ = stmp[d].ap.rearrange("p (h q) -> p h q", h=2)
                    P.tt(t3_, s3_, Es_[:, 2:4].unsqueeze(2).to_broadcast([128, 2, 64]), ALU.mult, [st[d], Es_], [stmp[d]], eng="pool")
                    P.tt(st[d].ap, stmp[d].ap, psU[:, 0:128], ALU.add, [stmp[d], psU], [st[d]])
                if step < 31:
                    P.act(stb[d].ap, st[d].ap, AF.Copy, [st[d]], [stb[d]])
                done = (slot == 3) if d == 0 else (slot == 0)
                if done:
                    t0 = tix * 512
                    iam_first = (tix <= 3) if d == 0 else (tix >= 4)
                    if iam_first:
                        P.act(yd[:, t0:t0 + 512], psY[:, :], AF.Copy, [psY], [yd])
                    else:
                        P.tt(yd[:, t0:t0 + 512], yd[:, t0:t0 + 512], psY[:, :], ALU.add, [psY, yd], [yd])
            nps.banks = [6, 7]
            for d_ in range(2):
                pre1(0, d_)
            for d_ in range(2):
                pre1(1, d_)
            for d_ in range(2):
                pre2(0, d_)
            for step in range(32):
                for d_ in range(2):
                    if step + 2 < 32:
                        pre1(step + 2, d_)
                for d_ in range(2):
                    if step + 1 < 32:
                        pre2(step + 1, d_)
                for d_ in range(2):
                    post(step, d_)
            nps.banks = [2, 3, 4, 5, 6, 7]
            for tt in range(NT):
                sl = slice(tt * 512, (tt + 1) * 512)
                P.stt(fin.ap, xTb[:, sl], dcol[l][:, b:b + 1], yd[:, sl], ALU.mult, ALU.add, [xTb, dcol[l], yd], [fin])
                psz = nps()
                inproj(wz, 128, tt, psz)
                silu_from_ps(psz, fin2.ap, fin2, e_t)
                P.tt(fin.ap, fin.ap, fin2.ap, ALU.mult, [fin, fin2], [fin])
                P.act(fin2.ap, fin.ap, AF.Square, [fin], [fin2])
                ps = nps()
                P.mm(ps[:, :], ONE, fin2.ap, True, True, [KC, fin2], [ps])
                if b == 0:
                    P.cp(ssq[:, sl], ps[:, :], [ps], [ssq])
                else:
                    P.tt(ssq[:, sl], ssq[:, sl], ps[:, :], ALU.add, [ssq, ps], [ssq])
                o_ = ost[tt % 2]
                P.act(o_.ap, fin.ap, AF.Identity, [fin, colA[l]], [o_], scale=colA[l][:, 64 + b:65 + b])
                P.dma("sp", yb_d[b, :, sl], o_.ap, [o_], [ybt[b]], o_)


def hg_phase(P, A, nps, l, env):
    KC, ID, IDB, ONE = env["KC"], env["ID"], env["IDB"], env["ONE"]
    hv, hTt, win_d, yb_d, ybt = env["hv"], env["hTt"], env["win_d"], env["yb_d"], env["ybt"]
    M64, RM, LBt, LN1, colA = env["M64"], env["RM"], env["LBt"], env["LN1"], env["colA"]
    c_one, c_heps, c_rmask = env["c_one"], env["c_heps"], env["c_rmask"]
    inproj, silu_from_ps = env["inproj"], env["silu_from_ps"]
    PSB = env["PS"]
    vz = [[A.alloc(128, BF16, "vz") for _ in range(2)] for d in range(2)]
    w5 = A.alloc(8 * 640, BF16, "w5")
    w5v = w5.ap.rearrange("p (k n) -> p k n", k=8)
    vtok = A.alloc(L, BF16, "vtok")
    qd = [A.alloc(L, BF16, "qd%d" % d) for d in range(2)]
    kd = [A.alloc(L, BF16, "kd%d" % d) for d in range(2)]
    kdt = [A.alloc(L, BF16, "kdt%d" % d) for d in range(2)]
    yd = A.alloc(L, F32, "yd")
    dvec = [A.alloc(3 * 128, F32, "dvec%d" % d) for d in range(2)]
    e_t = A.alloc(512, F32, "e")
    t1 = A.alloc(512, F32, "t1")
    L1 = A.alloc(512, F32, "L1")
    L2 = A.alloc(512, F32, "L2")
    r_ = A.alloc(512, F32, "r")
    Ac = A.alloc(512, F32, "Ac")
    E = A.alloc(512, F32, "E")
    e1 = A.alloc(512, F32, "e1")
    tot = A.alloc(16, F32, "tot")
    tm8 = A.alloc(16, F32, "tm8")
    S = [A.alloc(128, F32, "S%d" % d) for d in range(2)]
    Stmp2 = [[A.alloc(128, F32, "Up") for _ in range(2)] for d in range(2)]
    Smid = [A.alloc(128, BF16, "Smid%d" % d) for d in range(2)]
    am = [[A.alloc(32, BF16, "am") for _ in range(2)] for d in range(2)]
    ost = [A.alloc(512, BF16, "ost%d" % i) for i in range(2)]
    fin = A.alloc(512, F32, "fin")
    fin2 = A.alloc(512, F32, "fin2")
    rsd = A.alloc(512, F32, "rsd")
    cols = (C_Q, C_FF, C_FB, C_I, C_G)
    for j in range(8):
        for bi, c0 in enumerate(cols):
            P.dma("pool", w5v[:, :, bi * 128:(bi + 1) * 128],
                  win_d[l, :, c0 + j * 128:c0 + (j + 1) * 128].rearrange("(k p) n -> p k n", p=128), [], [w5], w5)
        for i4 in range(8):
            ps = nps()
            for q in range(4):
                i = i4 * 4 + q
                for kc in range(8):
                    P.mm(ps[:, q * 128:(q + 1) * 128], hv[:, kc, i * 128:(i + 1) * 128], w5v[:, kc, 384:512],
                         kc == 0, kc == 7, [w5, hTt[i // 4]], [ps])
            P.act(vtok[:, i4 * 512:(i4 + 1) * 512], ps[:, :], AF.Copy, [ps], [vtok])
        for tt in range(NT):
            sl = slice(tt * 512, (tt + 1) * 512)
            psq = nps()
            inproj(w5, 128, tt, psq, col0=0)
            for d in range(2):
                psf = nps()
                inproj(w5, 128, tt, psf, col0=128 * (1 + d))
                li = d * 32 + l * 8 + j
                lbc = LBt[:, li:li + 1]
                lnc = LN1[:, li:li + 1]
                P.act(e_t.ap, psf[:, :], AF.Exp, [psf], [e_t], scale=-1.0)
                P.act(L2.ap, e_t.ap, AF.Ln, [e_t, KC], [L2], bias=c_one)
                P.act(t1.ap, L2.ap, AF.Exp, [L2], [t1], scale=-1.0)
                P.act(L1.ap, e_t.ap, AF.Ln, [e_t, LBt, KC], [L1], bias=c_one, scale=lbc)
                P.tt(L1.ap, L1.ap, L2.ap, ALU.subtract, [L1, L2], [L1])
                P.tt(r_.ap, e_t.ap, t1.ap, ALU.mult, [e_t, t1], [r_])
                a3 = Ac.ap.rearrange("p (c q) -> p c q", q=32)
                l3 = L1.ap.rearrange("p (c q) -> p c q", q=32)
                if d == 0:
                    P.scan(Ac.ap, RM, L1.ap, ALU.mult, ALU.add, [KC, L1], [Ac])
                    mref = 15
                    P.cp(tot.ap, a3[:, :, 31], [Ac], [tot])
                else:
                    P.ms(Ac[:, 0:1], 0.0, [Ac])
                    P.scan(Ac[:, 1:512], L1[:, 0:511], RM[:, 1:512], ALU.add, ALU.mult, [KC, L1, Ac], [Ac])
                    mref = 16
                    P.tt(tot.ap, a3[:, :, 31], l3[:, :, 31], ALU.add, [Ac, L1], [tot])
                e3 = E.ap.rearrange("p (c q) -> p c q", q=32)
                P.tt(e3, a3, a3[:, :, mref:mref + 1].to_broadcast([128, 16, 32]), ALU.subtract, [Ac], [E])
                dv = dvec[d].ap.rearrange("p (a c) -> p a c", a=3)
                csl = slice(tt * 16, (tt + 1) * 16)
                P.tt(tm8.ap, tot.ap, a3[:, :, mref], ALU.subtract, [tot, Ac], [tm8])
                if d == 0:
                    P.act(dv[:, 0, csl], a3[:, :, mref], AF.Exp, [Ac], [dvec[d]])
                    P.act(dv[:, 1, csl], tm8.ap, AF.Exp, [tm8], [dvec[d]])
                else:
                    P.act(dv[:, 0, csl], tm8.ap, AF.Exp, [tm8], [dvec[d]])
                    P.act(dv[:, 1, csl], a3[:, :, mref], AF.Exp, [Ac], [dvec[d]])
                P.act(dv[:, 2, csl], tot.ap, AF.Exp, [tot], [dvec[d]])
                sq_, sk_ = (1.0, -1.0) if d == 0 else (-1.0, 1.0)
                P.act(e1.ap, E.ap, AF.Exp, [E], [e1], scale=sq_)
                P.tt(qd[d][:, sl], psq[:, :], e1.ap, ALU.mult, [psq, e1], [qd[d]])
                P.act(e1.ap, E.ap, AF.Exp, [E, LN1], [e1], bias=lnc, scale=sk_)
                P.tt(kd[d][:, sl], r_.ap, e1.ap, ALU.mult, [r_, e1], [kd[d]])
        for d in range(2):
            for i8 in range(4):
                ps = nps()
                pb = ps.ap.bitcast(BF16)
                for q in range(8):
                    i = i8 * 8 + q
                    P.tr(pb[:, q * 128:(q + 1) * 128], kd[d][:, i * 128:(i + 1) * 128], IDB.ap, [kd[d], IDB], [ps])
                P.act(kdt[d][:, i8 * 1024:(i8 + 1) * 1024], pb[:, 0:1024], AF.Copy, [ps], [kdt[d]])
        psY = [PSB[0], PSB[1]]
        psUfix = [[PSB[2], PSB[3]], [PSB[4], PSB[5]]]
        nps.banks = [6, 7]

        def geo(step, d):
            c = step if d == 0 else 127 - step
            i, half = c // 4, c % 4
            hs = slice(64, 128) if half == 3 else slice(half * 32, half * 32 + 32)
            return c, i, half, hs

        def pre(step, d):
            c, i, half, hs = geo(step, d)
            bsel = step % 2
            cs = slice(c * 32, (c + 1) * 32)
            tl = slice(i * 128, (i + 1) * 128)
            am_ = am[d][bsel]
            psa = nps()
            if half == 3:
                P.mm(psa[hs, 0:32], kd[d][:, (c - 1) * 32:(c + 1) * 32], qd[d][:, cs], True, True, [kd[d], qd[d]], [psa])
                P.tt(am_[hs, :], psa[hs, 0:32], M64[d][hs, 96:128], ALU.mult, [psa, KC], [am_])
            else:
                P.mm(psa[hs, 0:32], kd[d][:, cs], qd[d][:, cs], True, True, [kd[d], qd[d]], [psa])
                P.tt(am_[hs, :], psa[hs, 0:32], M64[d][hs, hs], ALU.mult, [psa, KC], [am_])
            psU = psUfix[d][bsel]
            if half == 3:
                vz_ = vz[d][bsel]
                P.ts(vz_[hs, :], vtok[hs, tl], c_rmask[hs, :], None, ALU.mult, None, [vtok, KC], [vz_])
                P.mm(psU[:, 0:128], kdt[d][hs, tl], vz_[hs, :], True, True, [kdt[d], vz_], [psU])
            else:
                P.mm(psU[:, 0:128], kdt[d][hs, tl], vtok[hs, tl], True, True, [kdt[d], vtok], [psU])
            if step > 0:
                dvp = dvec[d].ap.rearrange("p (a c) -> p a c", a=3)
                P.act(Stmp2[d][bsel].ap, psU[:, 0:128], AF.Identity, [psU, dvec[d]], [Stmp2[d][bsel]], scale=dvp[:, 1, c:c + 1])

        def post(step, d):
            c, i, half, hs = geo(step, d)
            first = step == 0
            bsel = step % 2
            cs = slice(c * 32, (c + 1) * 32)
            tl = slice(i * 128, (i + 1) * 128)
            am_ = am[d][bsel]
            psU = psUfix[d][bsel]
            dv = dvec[d].ap.rearrange("p (a c) -> p a c", a=3)
            slot = c % 16
            o_ = psY[d][:, slot * 32:(slot + 1) * 32]
            P.mm(o_, vtok[hs, tl], am_[hs, :], True, first, [vtok, am_], [psY[d]])
            if not first:
                P.mm(o_, Smid[d].ap, qd[d][:, cs], False, True, [Smid[d], qd[d]], [psY[d]])
            if first:
                P.act(S[d].ap, psU[:, 0:128], AF.Identity, [psU, dvec[d]], [S[d]], scale=dv[:, 1, c:c + 1])
            else:
                Up_ = Stmp2[d][bsel]
                P.stt(S[d].ap, S[d].ap, dv[:, 2, c:c + 1], Up_.ap, ALU.mult, ALU.add, [S[d], dvec[d], Up_], [S[d]])
            cn = c + 1 if d == 0 else c - 1
            if 0 <= cn < 128:
                P.act(Smid[d].ap, S[d].ap, AF.Identity, [S[d], dvec[d]], [Smid[d]], scale=dv[:, 0, cn:cn + 1])
            done = (slot == 15) if d == 0 else (slot == 0)
            if done:
                tix = c // 16
                t0 = tix * 512
                iam_first = (tix <= 3) if d == 0 else (tix >= 4)
                if iam_first:
                    P.act(yd[:, t0:t0 + 512], psY[d][:, :], AF.Copy, [psY[d]], [yd])
                else:
                    P.tt(yd[:, t0:t0 + 512], yd[:, t0:t0 + 512], psY[d][:, :], ALU.add, [psY[d], yd], [yd])
        pre(0, 0)
        pre(0, 1)
        for step in range(128):
            if step + 1 < 128:
                pre(step + 1, 0)
                pre(step + 1, 1)
            post(step, 0)
            post(step, 1)
        nps.banks = [2, 3, 4, 5, 6, 7]
        for tt in range(NT):
            sl = slice(tt * 512, (tt + 1) * 512)
            P.act(fin2.ap, yd[:, sl], AF.Square, [yd], [fin2])
            ps = nps()
            P.mm(ps[:, :], ONE, fin2.ap, True, True, [KC, fin2], [ps])
            P.rsqrt(rsd.ap, ps[:, :], c_heps, [ps, KC], [rsd])
            P.tt(fin.ap, yd[:, sl], rsd.ap, ALU.mult, [yd, rsd], [fin])
            P.ts(fin.ap, fin.ap, colA[l][:, 72:73], float(np.sqrt(128.0)), ALU.mult, ALU.mult, [fin, colA[l]], [fin])
            psg = nps()
            inproj(w5, 128, tt, psg, col0=512)
            silu_from_ps(psg, fin2.ap, fin2, e_t)
            o_ = ost[tt % 2]
            P.tt(o_.ap, fin.ap, fin2.ap, ALU.mult, [fin, fin2], [o_])
            P.dma("sp", yb_d[8 + j, :, sl], o_.ap, [o_], [ybt[8 + j]], o_)


def prep_inputs(inp):
    f = lambda a: np.ascontiguousarray(np.asarray(a, dtype=np.float32))
    vecA = np.concatenate([
        f(inp["ada_b"]).reshape(DEPTH, 48, 128), f(inp["norm_mix_w"]).reshape(DEPTH, 8, 128),
        f(inp["norm_mlp_w"]).reshape(DEPTH, 8, 128), f(inp["ssd_norm_w"]).reshape(DEPTH, 8, 128),
        f(inp["hg_norm_w"]).reshape(DEPTH, 1, 128)], axis=1)
    vecB = np.concatenate([f(inp["conv_w"]).reshape(DEPTH, 60, 128), f(inp["conv_b"]).reshape(DEPTH, 12, 128)], axis=1)
    vecC = np.concatenate([f(inp["hg_lower_bounds"]).reshape(64, 128), f(inp["final_norm_w"]).reshape(8, 128)], axis=0)
    shared = {
        "ada_w": f(inp["ada_w"]), "vecA": f(vecA), "vecB": f(vecB), "vecC": f(vecC),
        "dtb": f(inp["dt_bias"]).reshape(DEPTH, 32), "alog": f(inp["a_log"]).reshape(DEPTH, 32),
        "dsk": f(np.ascontiguousarray(f(inp["d_skip"]).reshape(DEPTH, 8, 2).transpose(0, 2, 1))), "w_in": f(inp["w_in"]), "w_out": f(inp["w_out"]),
        "w_up": f(inp["w_up"]), "w_down": f(inp["w_down"]), "consts": make_consts(),
    }
    x = f(inp["x"])
    c = f(inp["c"])
    maps = []
    for b in range(x.shape[0]):
        m = dict(shared)
        m["x"] = np.ascontiguousarray(x[b])
        m["c"] = np.ascontiguousarray(c[b].reshape(8, 128))
        maps.append(m)
    return maps


def kernel(**inputs):
    maps = prep_inputs(inputs)
    nc, _ = build()
    res = run_bass_kernel_spmd(nc, maps, core_ids=list(range(8)))
    return np.stack([np.asarray(r["out"], dtype=np.float32) for r in res.results], axis=0)
```
